# Optimizing a Trainium2 kernel written in Bass

```python
import math
import jax, jax.numpy as jnp
from jax import lax
import numpy as np

D_MODEL = 1024
BATCH = 4
SEQ = 8192
DEPTH = 2
DEC_BATCH = 2
DEC_SEQ = 16384
PAST_LEN = 128

HEAD_DIM = 64
BLOCK = 128
GRID_W = 64
LN_EPS = 1e-5
RMS_EPS = 1e-6
ROPE_THETA = 500000.0
ROPE_DIMS = HEAD_DIM // 4
AXIAL_THETA = 10000.0
A_GROUPS = 8
A_WIDTH = A_GROUPS * HEAD_DIM
A_CHUNK = 128
B_HEADS = 4
B_QK_DIM = HEAD_DIM
B_V_DIM = 2 * HEAD_DIM
B_WIDTH = B_HEADS * B_V_DIM
B_LAYER_IDX = 0
B_LAMBDA_INIT = 0.8 - 0.6 * math.exp(-0.3 * B_LAYER_IDX)
C_Q_HEADS = 8
C_KV_HEADS = 2
C_GROUP = C_Q_HEADS // C_KV_HEADS
C_WIDTH = C_Q_HEADS * HEAD_DIM
D_PATTERNS = ((128, 1), (512, 4), (2048, 16))
D_N_PAT = 3
D_SLOTS = 4
D_N_KEYS = 129
D_WIDTH = D_SLOTS * HEAD_DIM
D_FF = 2816
CONV_W = 3
ALPHA = (2 * DEPTH) ** 0.25
BETA = (8 * DEPTH) ** -0.25
IN0 = 2 * A_WIDTH + B_HEADS * (2 * 2 * B_QK_DIM + B_V_DIM)
IN1 = C_WIDTH + 2 * C_KV_HEADS * HEAD_DIM + 3 * D_N_PAT * D_WIDTH
OUT0 = A_WIDTH + B_WIDTH
OUT1 = C_WIDTH + D_WIDTH

kernel_name = "hybrid_bidir_encoder_gmlp_diffattn_axialgqa_dilated"


def layer_norm(x, g, b):
    xf = x.astype(jnp.float32)
    mu = jnp.mean(xf, -1, keepdims=True)
    var = jnp.mean(jnp.square(xf - mu), -1, keepdims=True)
    return ((xf - mu) * lax.rsqrt(var + LN_EPS) * g + b).astype(x.dtype)


def rms_norm(x, g):
    xf = x.astype(jnp.float32)
    return (xf * lax.rsqrt(jnp.mean(xf * xf, -1, keepdims=True) + RMS_EPS) * g).astype(x.dtype)


def rope_cos_sin(pos, n_dims, theta):
    inv = jnp.power(theta, -jnp.arange(0, n_dims, 2, dtype=jnp.float32) / n_dims)
    ang = pos.astype(jnp.float32)[:, None] * inv[None, :]
    return jnp.cos(ang), jnp.sin(ang)


def apply_rope(x, cos, sin):
    half = cos.shape[-1]
    x1, x2, rest = x[..., :half], x[..., half:2 * half], x[..., 2 * half:]
    cos = cos.astype(x.dtype)
    sin = sin.astype(x.dtype)
    return jnp.concatenate([x1 * cos - x2 * sin, x2 * cos + x1 * sin, rest], -1)


def expand_pos(c, n_mid):
    return c.reshape(c.shape[0], *([1] * n_mid), c.shape[-1])


def to_blocks(x):
    b, s = x.shape[:2]
    return jnp.moveaxis(x.reshape(b, s // BLOCK, BLOCK, *x.shape[2:]), 1, 0)


def from_blocks(y):
    n, b = y.shape[:2]
    return jnp.moveaxis(y, 0, 1).reshape(b, n * y.shape[2], *y.shape[3:])


def gmlp_chunk_mixer(z, ln_g, ln_b, w_s, b_s):
    u, v = jnp.split(z, 2, -1)
    v = layer_norm(v, ln_g, ln_b)
    bsz, s, _ = v.shape
    v = v.reshape(bsz, s // A_CHUNK, A_CHUNK, A_GROUPS, HEAD_DIM)
    mixed = jnp.einsum('gpq,bnqgc->bnpgc', w_s, v) + b_s.T[:, :, None]
    return u * mixed.reshape(bsz, s, A_WIDTH)


def diff_attention(q, k, v, lam_q1, lam_k1, lam_q2, lam_k2, subln_g, cos_p, sin_p):
    bsz, s = q.shape[:2]
    c5, s5 = expand_pos(cos_p, 2), expand_pos(sin_p, 2)
    q = apply_rope(q, c5, s5)
    k = apply_rope(k, c5, s5)
    f32 = jnp.float32
    lam = (jnp.exp(jnp.sum(lam_q1.astype(f32) * lam_k1.astype(f32)))
           - jnp.exp(jnp.sum(lam_q2.astype(f32) * lam_k2.astype(f32))) + B_LAMBDA_INIT)
    scale = B_QK_DIM ** -0.5

    def block(qb):
        sc = jnp.einsum('bqhmd,bkhmd->bhmqk', qb, k).astype(f32) * scale
        p = jax.nn.softmax(sc, -1)
        w = (p[:, :, 0] - lam * p[:, :, 1]).astype(v.dtype)
        return jnp.einsum('bhqk,bkhe->bqhe', w, v)

    o = from_blocks(lax.map(block, to_blocks(q)))
    o = rms_norm(o, subln_g) * (1.0 - B_LAMBDA_INIT)
    return o.reshape(bsz, s, B_WIDTH)


def axial_gqa(q, k, v, qn_g, kn_g, cos_r, sin_r, cos_c, sin_c):
    bsz, s = q.shape[:2]
    half = HEAD_DIM // 2
    q = rms_norm(q, qn_g)
    k = rms_norm(k, kn_g)

    def axial(x, n_mid):
        return jnp.concatenate([
            apply_rope(x[..., :half], expand_pos(cos_r, n_mid), expand_pos(sin_r, n_mid)),
            apply_rope(x[..., half:], expand_pos(cos_c, n_mid), expand_pos(sin_c, n_mid))], -1)

    q = axial(q, 2)
    k = axial(k, 1)
    scale = HEAD_DIM ** -0.5

    def block(qb):
        sc = jnp.einsum('bqkgd,bskd->bkgqs', qb, k).astype(jnp.float32) * scale
        p = jax.nn.softmax(sc, -1).astype(v.dtype)
        return jnp.einsum('bkgqs,bskd->bqkgd', p, v)

    o = from_blocks(lax.map(block, to_blocks(q)))
    return o.reshape(bsz, s, C_WIDTH)


def dilated_attention(q, k, v, cos_p, sin_p):
    bsz, s = q.shape[:2]
    c5, s5 = expand_pos(cos_p, 2), expand_pos(sin_p, 2)
    q = apply_rope(q, c5, s5)
    k = apply_rope(k, c5, s5)
    offsets = jnp.asarray(np.stack([np.arange(-(w // 2), w // 2 + 1, r) for w, r in D_PATTERNS]).astype(np.int32))
    p_idx = jnp.arange(D_N_PAT)[:, None, None]
    scale = HEAD_DIM ** -0.5
    neg = jnp.float32(-1e30)

    def block(args):
        qb, i = args
        t = i * BLOCK + jnp.arange(BLOCK)
        idx = t[None, :, None] + offsets[:, None, :]
        valid = (idx >= 0) & (idx < s)
        idx = jnp.clip(idx, 0, s - 1)
        kg = k[:, idx, p_idx]
        vg = v[:, idx, p_idx]
        sc = jnp.einsum('bqphd,bpqjhd->bhqpj', qb, kg).astype(jnp.float32) * scale
        sc = jnp.where(jnp.transpose(valid, (1, 0, 2))[None, None], sc, neg)
        w = jax.nn.softmax(sc.reshape(bsz, D_SLOTS, BLOCK, D_N_PAT * D_N_KEYS), -1)
        w = w.reshape(sc.shape).astype(v.dtype)
        return jnp.einsum('bhqpj,bpqjhd->bqhd', w, vg)

    o = from_blocks(lax.map(block, (to_blocks(q), jnp.arange(s // BLOCK))))
    return o.reshape(bsz, s, D_WIDTH)


def mixer_ab(x, w_in, a_ln_g, a_ln_b, a_ws, a_bs, lq1, lk1, lq2, lk2, subln_g, w_out, cos_p, sin_p):
    bsz, s, _ = x.shape
    h = x @ w_in
    n_a = 2 * A_WIDTH
    n_qk = B_HEADS * 2 * B_QK_DIM
    z_a, q_b, k_b, v_b = jnp.split(h, [n_a, n_a + n_qk, n_a + 2 * n_qk], -1)
    ya = gmlp_chunk_mixer(jax.nn.gelu(z_a), a_ln_g, a_ln_b, a_ws, a_bs)
    yb = diff_attention(q_b.reshape(bsz, s, B_HEADS, 2, B_QK_DIM),
                        k_b.reshape(bsz, s, B_HEADS, 2, B_QK_DIM),
                        v_b.reshape(bsz, s, B_HEADS, B_V_DIM),
                        lq1, lk1, lq2, lk2, subln_g, cos_p, sin_p)
    return jnp.concatenate([ya, yb], -1) @ w_out


def mixer_cd(x, w_in, qn_g, kn_g, w_out, cos_p, sin_p, cos_r, sin_r, cos_c, sin_c):
    bsz, s, _ = x.shape
    h = x @ w_in
    nq = C_WIDTH
    nkv = C_KV_HEADS * HEAD_DIM
    q_c, k_c, v_c, qkv_d = jnp.split(h, [nq, nq + nkv, nq + 2 * nkv], -1)
    yc = axial_gqa(q_c.reshape(bsz, s, C_KV_HEADS, C_GROUP, HEAD_DIM),
                   k_c.reshape(bsz, s, C_KV_HEADS, HEAD_DIM),
                   v_c.reshape(bsz, s, C_KV_HEADS, HEAD_DIM),
                   qn_g, kn_g, cos_r, sin_r, cos_c, sin_c)
    qkv_d = qkv_d.reshape(bsz, s, 3, D_N_PAT, D_SLOTS, HEAD_DIM)
    yd = dilated_attention(qkv_d[:, :, 0], qkv_d[:, :, 1], qkv_d[:, :, 2], cos_p, sin_p)
    return jnp.concatenate([yc, yd], -1) @ w_out


def conv_ffn(x, w_up, conv_w, conv_b, w_down):
    h = x @ w_up
    h = lax.conv_general_dilated(h, conv_w[:, None, :], window_strides=(1,), padding=((1, 1),),
                                 dimension_numbers=('NWC', 'WIO', 'NWC'),
                                 feature_group_count=h.shape[-1]) + conv_b
    a, g = jnp.split(h, 2, -1)
    return (jax.nn.gelu(g) * a) @ w_down


def trunk(x, mix_params, norm_params, ffn_params):
    s = x.shape[1]
    rows = s // GRID_W
    pos = jnp.arange(s)
    row_id = jnp.repeat(jnp.arange(rows), GRID_W)
    col_id = jnp.tile(jnp.arange(GRID_W), rows)
    cos_p, sin_p = rope_cos_sin(pos, ROPE_DIMS, ROPE_THETA)
    cos_r, sin_r = rope_cos_sin(row_id, HEAD_DIM // 2, AXIAL_THETA)
    cos_c, sin_c = rope_cos_sin(col_id, HEAD_DIM // 2, AXIAL_THETA)
    for layer in range(DEPTH):
        if layer % 2 == 0:
            y = mixer_ab(x, *mix_params[layer], cos_p, sin_p)
        else:
            y = mixer_cd(x, *mix_params[layer], cos_p, sin_p, cos_r, sin_r, cos_c, sin_c)
        g1, b1, g2, b2 = norm_params[layer]
        x = layer_norm(ALPHA * x + y, g1, b1)
        x = layer_norm(ALPHA * x + conv_ffn(x, *ffn_params[layer]), g2, b2)
    return x


def setup_inputs(seed: int = 0) -> dict:
    key = jax.random.key(seed)
    ks = iter(jax.random.split(key, 64))

    def nrm(shape, scale):
        return jax.random.normal(next(ks), shape, jnp.float32) * scale

    def gain(n):
        return 1.0 + nrm((n,), 0.02)

    def bias(n):
        return nrm((n,), 0.02)

    d = D_MODEL
    return {
        "x_prompt": nrm((BATCH, SEQ, d), 1.0),
        "x_sample": nrm((DEC_BATCH, DEC_SEQ, d), 1.0),
        "w_in0": nrm((d, IN0), d ** -0.5),
        "a_ln_g": gain(A_WIDTH),
        "a_ln_b": bias(A_WIDTH),
        "a_ws": nrm((A_GROUPS, A_CHUNK, A_CHUNK), A_CHUNK ** -0.5),
        "a_bs": 1.0 + nrm((A_GROUPS, A_CHUNK), 0.02),
        "b_lam_q1": nrm((B_QK_DIM,), 0.1),
        "b_lam_k1": nrm((B_QK_DIM,), 0.1),
        "b_lam_q2": nrm((B_QK_DIM,), 0.1),
        "b_lam_k2": nrm((B_QK_DIM,), 0.1),
        "b_subln_g": gain(B_V_DIM),
        "w_out0": nrm((OUT0, d), BETA * OUT0 ** -0.5),
        "ln0a_g": gain(d),
        "ln0a_b": bias(d),
        "w_up0": nrm((d, 2 * D_FF), d ** -0.5),
        "conv_w0": nrm((CONV_W, 2 * D_FF), CONV_W ** -0.5),
        "conv_b0": bias(2 * D_FF),
        "w_down0": nrm((D_FF, d), BETA * D_FF ** -0.5),
        "ln0b_g": gain(d),
        "ln0b_b": bias(d),
        "w_in1": nrm((d, IN1), d ** -0.5),
        "c_qnorm_g": gain(HEAD_DIM),
        "c_knorm_g": gain(HEAD_DIM),
        "w_out1": nrm((OUT1, d), BETA * OUT1 ** -0.5),
        "ln1a_g": gain(d),
        "ln1a_b": bias(d),
        "w_up1": nrm((d, 2 * D_FF), d ** -0.5),
        "conv_w1": nrm((CONV_W, 2 * D_FF), CONV_W ** -0.5),
        "conv_b1": bias(2 * D_FF),
        "w_down1": nrm((D_FF, d), BETA * D_FF ** -0.5),
        "ln1b_g": gain(d),
        "ln1b_b": bias(d),
    }


def reference(x_prompt, x_sample,
              w_in0, a_ln_g, a_ln_b, a_ws, a_bs, b_lam_q1, b_lam_k1, b_lam_q2, b_lam_k2, b_subln_g,
              w_out0, ln0a_g, ln0a_b, w_up0, conv_w0, conv_b0, w_down0, ln0b_g, ln0b_b,
              w_in1, c_qnorm_g, c_knorm_g, w_out1, ln1a_g, ln1a_b, w_up1, conv_w1, conv_b1, w_down1,
              ln1b_g, ln1b_b):
    mix_params = [
        (w_in0, a_ln_g, a_ln_b, a_ws, a_bs, b_lam_q1, b_lam_k1, b_lam_q2, b_lam_k2, b_subln_g, w_out0),
        (w_in1, c_qnorm_g, c_knorm_g, w_out1),
    ]
    norm_params = [(ln0a_g, ln0a_b, ln0b_g, ln0b_b), (ln1a_g, ln1a_b, ln1b_g, ln1b_b)]
    ffn_params = [(w_up0, conv_w0, conv_b0, w_down0), (w_up1, conv_w1, conv_b1, w_down1)]
    y_prompt = trunk(x_prompt, mix_params, norm_params, ffn_params)
    y_sample = trunk(x_sample, mix_params, norm_params, ffn_params)
    return (y_prompt, y_sample)
```

```python
import math
from contextlib import ExitStack

import numpy as np
import concourse.bass as bass
import concourse.mybir as mybir
from concourse.bass_utils import run_bass_kernel_spmd

F32 = mybir.dt.float32
BF16 = mybir.dt.bfloat16
AF = mybir.ActivationFunctionType
ALU = mybir.AluOpType
AX = mybir.AxisListType

D = 1024
DFF = 2816
NFC = 44
ALPHA = (2 * 2) ** 0.25
LAMBDA_INIT = 0.8 - 0.6 * math.exp(-0.3 * 0)
LN_EPS = 1e-5
RMS_EPS = 1e-6
NEG = -30000.0
HALO = 1024
PATTERNS = ((128, 1), (512, 4), (2048, 16))


class T:
    __slots__ = ("w", "r", "box", "dram", "name")

    def __init__(self, name="", dram=False):
        self.w = {}
        self.r = {}
        self.box = None
        self.dram = dram
        self.name = name


class SemBox:
    __slots__ = ("sem", "cnt")

    def __init__(self, sem):
        self.sem = sem
        self.cnt = 0


class Ctx:
    def __init__(self, nc):
        self.nc = nc
        self.E = {"pe": nc.tensor, "act": nc.scalar, "dve": nc.vector, "pool": nc.gpsimd, "sp": nc.sync}
        self.cbox = {k: SemBox(nc.alloc_semaphore("c_" + k)) for k in ("pe", "act", "dve", "pool")}
        self.waited = {k: {} for k in self.E}
        self.free_boxes = []
        self.live_boxes = []
        self.all_boxes = []
        self.ninst = 0

    def _need(self, eng, deps):
        best = {}
        pesem = self.cbox["pe"].sem
        for d in deps:
            for key, (sem, val) in d.items():
                if eng == "pe" and sem is pesem:
                    continue
                cur = best.get(key)
                if cur is None or cur[1] < val:
                    best[key] = (sem, val)
        w = self.waited[eng]
        e = self.E[eng]
        for key, (sem, val) in best.items():
            if w.get(key, 0) >= val:
                continue
            e.wait_ge(sem, val)
            w[key] = val

    def _record(self, tok, reads, writes):
        key = id(tok[0])
        for t in reads:
            t.r[key] = tok
        for t in writes:
            if t.dram:
                t.w[key] = tok
            else:
                t.w = {key: tok}
                t.r = {}

    def op(self, eng, fn, reads=(), writes=()):
        deps = []
        for t in reads:
            deps.append(t.w)
        for t in writes:
            deps.append(t.w)
            deps.append(t.r)
        self._need(eng, deps)
        inst = fn(self.E[eng])
        box = self.cbox[eng]
        box.cnt += 1
        inst.then_inc(box.sem, 1)
        self._record((box.sem, box.cnt), reads, writes)
        self.ninst += 1
        return inst

    def _box(self, t):
        if t.box is None:
            if self.free_boxes:
                t.box = self.free_boxes.pop()
            else:
                t.box = SemBox(self.nc.alloc_semaphore("d%d" % len(self.all_boxes)))
                self.all_boxes.append(t.box)
            self.live_boxes.append(t.box)
        return t.box

    def dma(self, q, out, in_, reads=(), writes=(), owner=None, **kw):
        deps = []
        for t in reads:
            deps.append(t.w)
        for t in writes:
            if not t.dram:
                deps.append(t.w)
                deps.append(t.r)
        self._need(q, deps)
        box = self._box(owner)
        inst = self.E[q].dma_start(out=out, in_=in_, **kw)
        box.cnt += 16
        inst.then_inc(box.sem, 16)
        self._record((box.sem, box.cnt), reads, writes)
        self.ninst += 1
        return inst

    def drain(self, eng, ts):
        self._need(eng, [t.w for t in ts] + [t.r for t in ts])

    def phase_barrier(self):
        tok = {}
        for b in list(self.cbox.values()) + self.all_boxes:
            if b.cnt > 0:
                tok[id(b.sem)] = (b.sem, b.cnt)
        for eng in self.E:
            self._need(eng, [tok])
        self.free_boxes.extend(self.live_boxes)
        self.live_boxes = []


class Tile:
    def __init__(self, nc, stack, name, shape, dtype):
        self.h = stack.enter_context(nc.sbuf_tensor(name, list(shape), dtype))
        self.t = T(name)

    def __getitem__(self, idx):
        return self.h[idx]


def build_program(Tk, debug=False, stop_after=None):
    NT = Tk // 128
    NG = Tk // 512
    NKEY = 2 * Tk
    NKB = NKEY // 128
    NST = Tk // 2048
    TP = Tk + 2 * HALO

    nc = bass.Bass("TRN2", target_bir_lowering=False, num_devices=8)
    cx = Ctx(nc)
    uid = [0]

    def ein(name, shape, dt=F32):
        return nc.dram_tensor(name, list(shape), dt, kind="ExternalInput").ap()

    def scr(name, shape, dt, shared=False):
        if shared:
            return nc.dram_tensor(name, list(shape), dt, kind="Internal", addr_space="Shared").ap()
        if debug and not name.startswith("W"):
            return nc.dram_tensor(name, list(shape), dt, kind="ExternalOutput").ap()
        return nc.dram_tensor(name, list(shape), dt, kind="Internal").ap()

    import os as _os
    SKIP = set(_os.environ.get("KDBG_SKIP", "").split(","))

    def done():
        cx.phase_barrier()
        gstack.close()
        return nc, cx

    x_in = ein("x", [Tk, D])
    tab_in = ein("tab", [Tk, 80])
    flags_in = ein("flags", [128, 8])
    ident_in = ein("ident", [128, 128])
    mask_in = ein("maskab", [128, 2, 128])
    w_in0_in = ein("w_in0", [D, 2560])
    wsT_in = ein("wsT", [128, 8, 128])
    bsT_in = ein("bsT", [128, 8])
    a_ln_g_in = ein("a_ln_g", [512])
    a_ln_b_in = ein("a_ln_b", [512])
    lam_in = ein("lam", [1, 256])
    subg_in = ein("subg", [128, 1])
    w_out0_in = ein("w_out0", [D, D])
    w_out1_in = ein("w_out1", [768, D])
    w_in1_in = ein("w_in1", [D, 3072])
    qng_in = ein("qng", [64])
    kng_in = ein("kng", [64])
    lnp_in = {}
    for nm in ("ln0a_g", "ln0a_b", "ln0b_g", "ln0b_b", "ln1a_g", "ln1a_b", "ln1b_g", "ln1b_b"):
        lnp_in[nm] = ein(nm, [D])
    wupR_in = [ein("w_upR%d" % l, [NFC, 128, 1024]) for l in range(2)]
    cw_in = [ein("convwT%d" % l, [128, NFC, 3]) for l in range(2)]
    cbias_in = [ein("convbT%d" % l, [128, NFC]) for l in range(2)]
    wdn_in = [ein("w_down%d" % l, [DFF, D]) for l in range(2)]
    y_out = nc.dram_tensor("y", [Tk, D], F32, kind="ExternalOutput").ap()

    Win0b = scr("Win0b", [D, 2560], BF16); tWin0b = T(dram=True)
    Wout0b = scr("Wout0b", [D, D], BF16); tWout0b = T(dram=True)
    Win1b = scr("Win1b", [D, 3072], BF16); tWin1b = T(dram=True)
    Wout1b = scr("Wout1b", [768, D], BF16); tWout1b = T(dram=True)
    WupRb = [scr("WupRb%d" % l, [NFC, 128, 1024], BF16) for l in range(2)]; tWupRb = [T(dram=True), T(dram=True)]
    Wdnb = [scr("Wdnb%d" % l, [DFF, D], BF16) for l in range(2)]; tWdnb = [T(dram=True), T(dram=True)]

    QT0 = scr("QT0", [4, 128, Tk], BF16); tQT0 = T(dram=True)
    L1a, L1b = 4 * 128 * Tk, Tk * 512
    LOC1 = scr("LOC1", [L1a + L1b], BF16); tLOC1 = T(dram=True)
    SH1 = scr("SH1", [2, L1a + L1b], BF16, shared=True); tSH1 = T(dram=True)
    KT0loc = LOC1[0:L1a].rearrange("(h p t) -> h p t", h=4, p=128)
    V0loc = LOC1[L1a:L1a + L1b].rearrange("(t n) -> t n", n=512)
    KT0sh = SH1[:, 0:L1a].rearrange("s (h p t) -> s h p t", h=4, p=128)
    V0sh = SH1[:, L1a:L1a + L1b].rearrange("s (t n) -> s t n", n=512)
    tKT0 = tSH1
    tV0 = tSH1
    CAT0T = scr("CAT0T", [1024, Tk], BF16); tCAT0 = T(dram=True)
    X1 = scr("X1", [Tk, D], F32); tX1 = T(dram=True)
    X1T = scr("X1T", [D, Tk], BF16); tX1T = T(dram=True)
    LOC2 = scr("LOC2", [2 * D], BF16); tLOC2 = T(dram=True)
    SH2 = scr("SH2", [2, 2 * D], BF16, shared=True); tSH2 = T(dram=True)
    X2 = scr("X2", [Tk, D], F32); tX2 = T(dram=True)
    Q1T = scr("Q1T", [4, 128, Tk], BF16); tQ1T = T(dram=True)
    L3a, L3b, L3c, L3d = 2 * 128 * Tk, Tk * 130, 2 * 6 * 128 * HALO, 2 * HALO * 780
    L3 = L3a + L3b + L3c + L3d
    LOC3 = scr("LOC3", [L3], BF16); tLOC3 = T(dram=True)
    SH3 = scr("SH3", [2, L3], BF16, shared=True); tSH3 = T(dram=True)
    K1Tloc = LOC3[0:L3a].rearrange("(k p t) -> k p t", k=2, p=128)
    V1loc = LOC3[L3a:L3a + L3b].rearrange("(t n) -> t n", n=130)
    KdTHloc = LOC3[L3a + L3b:L3a + L3b + L3c].rearrange("(w c p t) -> w c p t", w=2, c=6, p=128)
    VdHloc = LOC3[L3a + L3b + L3c:L3].rearrange("(w t n) -> w t n", w=2, n=780)
    K1Tsh = SH3[:, 0:L3a].rearrange("s (k p t) -> s k p t", k=2, p=128)
    V1sh = SH3[:, L3a:L3a + L3b].rearrange("s (t n) -> s t n", n=130)
    KdTHsh = SH3[:, L3a + L3b:L3a + L3b + L3c].rearrange("s (w c p t) -> s w c p t", w=2, c=6, p=128)
    VdHsh = SH3[:, L3a + L3b + L3c:L3].rearrange("s (w t n) -> s w t n", w=2, n=780)
    tK1T = tSH3
    tV1 = tSH3
    tKdTH = tSH3
    tVdH = tSH3
    QdT = scr("QdT", [6, 128, Tk], BF16); tQdT = T(dram=True)
    KdTP = scr("KdTP", [6, 128, TP], BF16); tKdTP = T(dram=True)
    VdP = scr("VdP", [TP, 780], BF16); tVdP = T(dram=True)
    CAT1T = scr("CAT1T", [768, Tk], BF16); tCAT1 = T(dram=True)
    X3 = scr("X3", [Tk, D], F32); tX3 = T(dram=True)
    X3T = scr("X3T", [D, Tk], BF16); tX3T = T(dram=True)
    LOC4 = scr("LOC4", [2 * D], BF16); tLOC4 = T(dram=True)
    SH4 = scr("SH4", [2, 2 * D], BF16, shared=True); tSH4 = T(dram=True)
    tY = T(dram=True)
    tIN = T(dram=True)

    ps = nc.alloc_psum_tensor("ps", [128, 8, 512], F32)
    pb = [T("pb%d" % i) for i in range(8)]

    def MM(out, lhsT, rhs, start, stop, R, W):
        cx.op("pe", lambda e: e.matmul(out, lhsT=lhsT, rhs=rhs, start=start, stop=stop), R, W)

    def ACT(out, in_, func, R, W, bias=None, scale=None):
        kw = {}
        if bias is not None:
            kw["bias"] = bias
        if scale is not None:
            kw["scale"] = scale
        cx.op("act", lambda e: e.activation(out=out, in_=in_, func=func, **kw), R, W)

    def TT(eng, out, in0, in1, op, R, W):
        cx.op(eng, lambda e: e.tensor_tensor(out=out, in0=in0, in1=in1, op=op), R, W)

    def TS(eng, out, in0, s1, s2, op0, op1, R, W):
        if op1 is None:
            cx.op(eng, lambda e: e.tensor_scalar(out=out, in0=in0, scalar1=s1, scalar2=None, op0=op0), R, W)
        else:
            cx.op(eng, lambda e: e.tensor_scalar(out=out, in0=in0, scalar1=s1, scalar2=s2, op0=op0, op1=op1), R, W)

    def STT(eng, out, in0, scalar, in1, op0, op1, R, W):
        cx.op(eng, lambda e: e.scalar_tensor_tensor(out=out, in0=in0, scalar=scalar, in1=in1, op0=op0, op1=op1), R, W)

    def CP(eng, out, in_, R, W):
        if eng == "act":
            ACT(out, in_, AF.Identity, R, W)
        else:
            cx.op(eng, lambda e: e.tensor_copy(out=out, in_=in_), R, W)

    def RECIP(out, in_, R, W):
        cx.op("dve", lambda e: e.reciprocal(out=out, in_=in_), R, W)

    def LD(out, in_, R, W, owner, q="sp", slow=False):
        if slow:
            cx.dma(q, out, in_, reads=R, writes=W, owner=owner, allow_slow_non_contiguous=True)
        else:
            cx.dma(q, out, in_, reads=R, writes=W, owner=owner)

    def ST(out, in_, R, W, owner, slow=False):
        if slow:
            cx.dma("pool", out, in_, reads=R, writes=W, owner=owner, allow_slow_non_contiguous=True)
        else:
            cx.dma("pool", out, in_, reads=R, writes=W, owner=owner)

    slot_v = nc.gpsimd.snap(nc.gpsimd.partition_id() % 2, min_val=0, max_val=1)
    pubT = T("pub")

    def publish(LOC, tLOC, SH, tSH):
        n = LOC.shape[0]
        a = 128 if n % 128 == 0 else 1
        cx.dma("pool", SH[bass.ds(slot_v, 1), :].rearrange("o (a b) -> (o a) b", a=a),
               LOC.rearrange("(a b) -> a b", a=a), reads=[tLOC], writes=[tSH], owner=pubT)
        cx.drain("pool", [tSH])
        cx.phase_barrier()
        nc.all_core_barrier()

    def mk(stack, name, shape, dt):
        uid[0] += 1
        return Tile(nc, stack, "%s_%d" % (name, uid[0]), shape, dt)

    gstack = ExitStack()
    ident = mk(gstack, "ident", [128, 128], BF16)
    ones_bf = mk(gstack, "ones_bf", [128, 128], BF16)
    ones_f = mk(gstack, "ones_f", [128, 128], F32)
    flags = mk(gstack, "flags", [128, 8], F32)
    ceps = mk(gstack, "ceps", [128, 4], F32)
    LD(ident[:], ident_in, [tIN], [ident.t], ident.t, q="pool")
    LD(flags[:], flags_in, [tIN], [flags.t], flags.t)
    cx.op("dve", lambda e: e.memset(ones_bf[:], 1.0), [], [ones_bf.t])
    cx.op("dve", lambda e: e.memset(ones_f[:], 1.0), [], [ones_f.t])
    cx.op("dve", lambda e: e.memset(ceps[:, 0:1], LN_EPS), [], [ceps.t])
    cx.op("dve", lambda e: e.memset(ceps[:, 1:2], RMS_EPS), [], [ceps.t])
    cx.op("dve", lambda e: e.memset(ceps[:, 2:3], 128.0 * RMS_EPS), [], [ceps.t])

    wdummy = T("wcast")

    def cast2d(dst, src, tdst, rows):
        R_ = src.shape[0]
        for r0 in range(0, R_, rows):
            r1 = min(R_, r0 + rows)
            ST(dst[r0:r1, :], src[r0:r1, :], [tIN], [tdst], wdummy)

    cast2d(Win0b, w_in0_in, tWin0b, 128)
    cast2d(Wout0b, w_out0_in, tWout0b, 128)
    cast2d(Win1b, w_in1_in, tWin1b, 128)
    cast2d(Wout1b, w_out1_in, tWout1b, 128)
    for l in range(2):
        for fc in range(NFC):
            ST(WupRb[l][fc], wupR_in[l][fc], [tIN], [tWupRb[l]], wdummy)
        cast2d(Wdnb[l], wdn_in[l], tWdnb[l], 128)

    if stop_after == "W":
        return done()

    def transposes(src_ap_fn, n, bank0, R, ):
        for j in range(n):
            b = bank0 + j // 4
            MM(ps[:, b, (j % 4) * 128:(j % 4) * 128 + 128], src_ap_fn(j), ident[:], True, True,
               R + [ident.t], [pb[b]])

    def layernorm(stk_tiles, r, out_ap, gB, bB, W):
        st_, mv, rs, tmp = stk_tiles
        cx.op("dve", lambda e: e.bn_stats(out=st_[:, 0:6], in_=r[:, 0:512]), [r.t], [st_.t])
        cx.op("dve", lambda e: e.bn_stats(out=st_[:, 6:12], in_=r[:, 512:1024]), [r.t], [st_.t])
        cx.op("dve", lambda e: e.bn_aggr(out=mv[:, 0:2], in_=st_[:, 0:12]), [st_.t], [mv.t])
        ACT(rs[:, 0:1], mv[:, 1:2], AF.Sqrt, [mv.t, ceps.t], [rs.t], bias=ceps[:, 0:1], scale=1.0)
        RECIP(rs[:, 0:1], rs[:, 0:1], [rs.t], [rs.t])
        TS("dve", tmp[:], r[:], mv[:, 0:1], rs[:, 0:1], ALU.subtract, ALU.mult, [r.t, mv.t, rs.t], [tmp.t])
        TT("pool", tmp[:], tmp[:], gB[:], ALU.mult, [tmp.t, gB.t], [tmp.t])
        TT("pool", out_ap, tmp[:], bB[:], ALU.add, [tmp.t, bB.t], W)

    def rope16(src3, dst3, nh, cos2, sin2, tmp, R, Wt):
        cb = cos2.unsqueeze(1).to_broadcast([128, nh, 8])
        sb_ = sin2.unsqueeze(1).to_broadcast([128, nh, 8])
        x1 = src3[:, :, 0:8]
        x2 = src3[:, :, 8:16]
        t = [tmp[:, i, 0:nh * 8].rearrange("p (h d) -> p h d", d=8) for i in range(4)]
        TT("dve", t[0], x1, cb, ALU.mult, R, [tmp.t])
        TT("dve", t[1], x2, sb_, ALU.mult, R, [tmp.t])
        TT("dve", t[2], x2, cb, ALU.mult, R, [tmp.t])
        TT("dve", t[3], x1, sb_, ALU.mult, R, [tmp.t])
        TT("dve", dst3[:, :, 0:8], t[0], t[1], ALU.subtract, [tmp.t], Wt)
        TT("dve", dst3[:, :, 8:16], t[2], t[3], ALU.add, [tmp.t], Wt)

    cx.phase_barrier()
    if "A" not in SKIP:
        with ExitStack() as st:
            w_in = mk(st, "w_in", [128, 8, 2560], BF16)
            LD(w_in[:], Win0b.rearrange("(kc p) n -> p kc n", p=128), [tWin0b], [w_in.t], w_in.t)
            wsT = mk(st, "wsT", [128, 8, 128], BF16)
            LD(wsT[:], wsT_in, [tIN], [wsT.t], wsT.t, q="pool")
            bsT = mk(st, "bsT", [128, 8], F32)
            LD(bsT[:], bsT_in, [tIN], [bsT.t], bsT.t)
            lng = mk(st, "lng", [128, 512], F32)
            lnb = mk(st, "lnb", [128, 512], F32)
            LD(lng[:], a_ln_g_in.partition_broadcast(128), [tIN], [lng.t], lng.t)
            LD(lnb[:], a_ln_b_in.partition_broadcast(128), [tIN], [lnb.t], lnb.t)
            xin = [mk(st, "xin", [128, 1024], F32) for _ in range(2)]
            tabt = [mk(st, "tab", [128, 80], F32) for _ in range(2)]
            xb = mk(st, "xb", [128, 1024], BF16)
            xT = mk(st, "xT", [128, 2, 512], BF16)
            uv = mk(st, "uv", [128, 2, 512], F32)
            stt_ = mk(st, "stt", [128, 12], F32)
            mv = mk(st, "mv", [128, 2], F32)
            rs = mk(st, "rs", [128, 1], F32)
            vtmp = mk(st, "vtmp", [128, 512], F32)
            vn = mk(st, "vn", [128, 512], BF16)
            mixt = mk(st, "mixt", [128, 512], F32)
            ya = mk(st, "ya", [128, 512], BF16)
            qk_r = mk(st, "qk_r", [128, 2, 512], BF16)
            rtmp = mk(st, "rtmp", [128, 4, 128], F32)
            vs = [mk(st, "vs", [128, 512], BF16) for _ in range(2)]
            QTs = [mk(st, "QTs", [128, 4, 512], BF16) for _ in range(2)]
            KTs = [mk(st, "KTs", [128, 4, 512], BF16) for _ in range(2)]
            YTs = [mk(st, "YTs", [128, 4, 512], BF16) for _ in range(2)]

            def loadA(i):
                LD(xin[i % 2][:], x_in[i * 128:(i + 1) * 128, :], [tIN], [xin[i % 2].t], xin[i % 2].t)
                LD(tabt[i % 2][:], tab_in[i * 128:(i + 1) * 128, :], [tIN], [tabt[i % 2].t], tabt[i % 2].t)

            loadA(0)
            for i in range(NT):
                if i + 1 < NT:
                    loadA(i + 1)
                sub = i % 4
                grp = i // 4
                g0 = grp * 512
                xi = xin[i % 2]
                tb = tabt[i % 2]
                CP("pool", xb[:], xi[:], [xi.t], [xb.t])
                transposes(lambda j: xb[:, j * 128:(j + 1) * 128], 8, 0, [xb.t])
                ACT(xT[:], ps[:, 0:2, :], AF.Identity, [pb[0], pb[1]], [xT.t])
                for cb in range(5):
                    for kc in range(8):
                        MM(ps[:, 2 + cb, :], xT[:, kc // 4, (kc % 4) * 128:(kc % 4) * 128 + 128],
                           w_in[:, kc, cb * 512:(cb + 1) * 512], kc == 0, kc == 7, [xT.t, w_in.t], [pb[2 + cb]])
                ACT(uv[:], ps[:, 2:4, :], AF.Gelu_apprx_tanh, [pb[2], pb[3]], [uv.t])
                cx.op("dve", lambda e: e.bn_stats(out=stt_[:, 0:6], in_=uv[:, 1, :]), [uv.t], [stt_.t])
                cx.op("dve", lambda e: e.bn_aggr(out=mv[:, 0:2], in_=stt_[:, 0:6]), [stt_.t], [mv.t])
                ACT(rs[:, 0:1], mv[:, 1:2], AF.Sqrt, [mv.t, ceps.t], [rs.t], bias=ceps[:, 0:1], scale=1.0)
                RECIP(rs[:, 0:1], rs[:, 0:1], [rs.t], [rs.t])
                TS("dve", vtmp[:], uv[:, 1, :], mv[:, 0:1], rs[:, 0:1], ALU.subtract, ALU.mult, [uv.t, mv.t, rs.t], [vtmp.t])
                TT("pool", vtmp[:], vtmp[:], lng[:], ALU.mult, [vtmp.t, lng.t], [vtmp.t])
                TT("pool", vn[:], vtmp[:], lnb[:], ALU.add, [vtmp.t, lnb.t], [vn.t])
                for g in range(8):
                    MM(ps[:, 7, g * 64:(g + 1) * 64], wsT[:, g, :], vn[:, g * 64:(g + 1) * 64], True, True,
                       [wsT.t, vn.t], [pb[7]])
                TT("dve", mixt[:].rearrange("p (g c) -> p g c", c=64), ps[:, 7, :].rearrange("p (g c) -> p g c", c=64),
                   bsT[:, :].unsqueeze(2).to_broadcast([128, 8, 64]), ALU.add, [pb[7], bsT.t], [mixt.t])
                TT("dve", ya[:], mixt[:], uv[:, 0, :], ALU.mult, [mixt.t, uv.t], [ya.t])
                ACT(qk_r[:], ps[:, 4:6, :], AF.Identity, [pb[4], pb[5]], [qk_r.t])
                src3 = ps[:, 4:6, :].rearrange("p b (h d) -> p (b h) d", d=64)
                dst3 = qk_r[:].rearrange("p b (h d) -> p (b h) d", d=64)
                rope16(src3, dst3, 16, tb[:, 0:8], tb[:, 8:16], rtmp, [pb[4], pb[5], tb.t], [qk_r.t])
                CP("dve", vs[i % 2][:], ps[:, 6, :], [pb[6]], [vs[i % 2].t])
                ST(V0loc[i * 128:(i + 1) * 128, :], vs[i % 2][:], [vs[i % 2].t], [tLOC1], vs[i % 2].t)
                qs, ks, ys = QTs[grp % 2], KTs[grp % 2], YTs[grp % 2]
                transposes(lambda j: qk_r[:, j // 4, (j % 4) * 128:(j % 4) * 128 + 128], 8, 0, [qk_r.t])
                CP("dve", qs[:, :, sub * 128:(sub + 1) * 128], ps[:, 0, :].rearrange("p (h t) -> p h t", t=128), [pb[0]], [qs.t])
                CP("act", ks[:, :, sub * 128:(sub + 1) * 128], ps[:, 1, :].rearrange("p (h t) -> p h t", t=128), [pb[1]], [ks.t])
                transposes(lambda j: ya[:, j * 128:(j + 1) * 128], 4, 2, [ya.t])
                CP("dve", ys[:, :, sub * 128:(sub + 1) * 128], ps[:, 2, :].rearrange("p (h t) -> p h t", t=128), [pb[2]], [ys.t])
                if sub == 3:
                    ST(QT0[:, :, g0:g0 + 512].rearrange("h p t -> p h t"), qs[:], [qs.t], [tQT0], qs.t)
                    ST(KT0loc[:, :, g0:g0 + 512].rearrange("h p t -> p h t"), ks[:], [ks.t], [tLOC1], ks.t)
                    ST(CAT0T[0:512, g0:g0 + 512].rearrange("(c p) t -> p c t", p=128), ys[:], [ys.t], [tCAT0], ys.t)
    if stop_after == "A0":
        return done()
    publish(LOC1, tLOC1, SH1, tSH1)
    if stop_after == "A":
        return done()

    def attention(n_outer, n_inner, QTd, tQ, KTsh_, tK, load_v, vshape, vcols, has_l, finalize):
        with ExitStack() as st:
            kt = [mk(st, "kt", [128, NKEY], BF16) for _ in range(2)]
            vt = [mk(st, "vt", [128, NKB] + vshape, BF16) for _ in range(2)]
            qts = [mk(st, "qts", [128, 512], BF16) for _ in range(3)]
            pts = [mk(st, "pts", [128, 2, 512], BF16) for _ in range(3)]
            fin = finalize(st)

            def load_kv(o):
                for s in range(2):
                    LD(kt[o % 2][:, s * Tk:(s + 1) * Tk], KTsh_[s, o, :, :], [tK], [kt[o % 2].t], kt[o % 2].t)
                load_v(o, vt[o % 2])

            units = [(o, i, j) for o in range(n_outer) for i in range(n_inner) for j in range(NG)]
            import os as _os
            if _os.environ.get("KDBG_UNITS"):
                units = units[:int(_os.environ["KDBG_UNITS"])]
            nkb_run = int(_os.environ.get("KDBG_NKB", NKB))
            nofin = bool(_os.environ.get("KDBG_NOFIN"))

            def load_q(u):
                o, i, j = units[u]
                qt_ = qts[u % 3]
                LD(qt_[:], QTd[o * n_inner + i, :, j * 512:(j + 1) * 512], [tQ], [qt_.t], qt_.t)

            load_kv(0)
            load_q(0)
            for u, (o, i, j) in enumerate(units):
                if i == 0 and j == 0 and o + 1 < n_outer:
                    load_kv(o + 1)
                if u + 1 < len(units):
                    load_q(u + 1)
                k_ = kt[o % 2]
                v_ = vt[o % 2]
                q_ = qts[u % 3]

                def qk(kb):
                    sb = (kb % 2) * 2
                    for m in range(2):
                        MM(ps[:, sb + m, :], k_[64 * m:64 * m + 64, kb * 128:(kb + 1) * 128], q_[64 * m:64 * m + 64, :],
                           True, True, [k_.t, q_.t], [pb[sb + m]])

                qk(0)
                for kb in range(nkb_run):
                    if kb + 1 < nkb_run:
                        qk(kb + 1)
                    sb = (kb % 2) * 2
                    p_ = pts[kb % 3]
                    s = kb // NT
                    ACT(p_[:], ps[:, sb:sb + 2, :], AF.Exp, [pb[sb], pb[sb + 1], flags.t], [p_.t],
                        bias=flags[:, s:s + 1], scale=0.125)
                    for m in range(2):
                        MM(ps[0:vcols, 4 + m, :], v_[:, kb, 0:vcols], p_[:, m, :], kb == 0, kb == nkb_run - 1, [v_.t, p_.t], [pb[4 + m]])
                        if has_l:
                            MM(ps[0:1, 6 + m, :], ones_bf[:, 0:1], p_[:, m, :], kb == 0, kb == nkb_run - 1,
                               [ones_bf.t, p_.t], [pb[6 + m]])
                if not nofin:
                    fin(o, i, j)

    if "B" not in SKIP:
        with ExitStack() as stB:
            lamv = mk(stB, "lamv", [128, 4, 64], F32)
            pr = mk(stB, "pr", [128, 2, 64], F32)
            ssum = mk(stB, "ssum", [128, 2], F32)
            esum = mk(stB, "esum", [128, 2], F32)
            nlam = mk(stB, "nlam", [128, 1], F32)
            subg = mk(stB, "subg", [128, 1], F32)
            gvec = mk(stB, "gvec", [128, 1], F32)
            LD(lamv[:].rearrange("p a b -> p (a b)"), lam_in.rearrange("o n -> (o n)").partition_broadcast(128), [tIN], [lamv.t], lamv.t)
            LD(subg[:], subg_in, [tIN], [subg.t], subg.t)
            TT("dve", pr[:], lamv[:, 0:2, :], lamv[:, 2:4, :], ALU.mult, [lamv.t], [pr.t])
            cx.op("dve", lambda e: e.tensor_reduce(out=ssum[:, 0:2], in_=pr[:], axis=AX.X, op=ALU.add), [pr.t], [ssum.t])
            ACT(esum[:], ssum[:], AF.Exp, [ssum.t], [esum.t])
            TT("dve", nlam[:], esum[:, 0:1], esum[:, 1:2], ALU.subtract, [esum.t], [nlam.t])
            TS("dve", nlam[:], nlam[:], LAMBDA_INIT, None, ALU.add, None, [nlam.t], [nlam.t])
            TS("dve", nlam[:], nlam[:], -1.0, None, ALU.mult, None, [nlam.t], [nlam.t])
            TS("dve", gvec[:], subg[:], (1.0 - LAMBDA_INIT) * math.sqrt(128.0), None, ALU.mult, None, [subg.t], [gvec.t])

            def load_v0(o, vtile):
                for s in range(2):
                    for c0 in range(0, NT, 16):
                        LD(vtile[:, s * NT + c0:s * NT + c0 + 16, :],
                           V0sh[s, c0 * 128:(c0 + 16) * 128, o * 128:(o + 1) * 128].rearrange("(b p) n -> p b n", p=128),
                           [tV0], [vtile.t], vtile.t)

            def finalize0(st):
                rl = mk(st, "rl", [128, 2, 512], F32)
                bcs = mk(st, "bcs", [128, 2, 512], F32)
                o1 = mk(st, "o1", [128, 512], F32)
                o2 = mk(st, "o2", [128, 512], F32)
                sq = mk(st, "sq", [128, 512], F32)
                ybt = [mk(st, "ybt", [128, 512], BF16) for _ in range(2)]
                cnt = [0]

                def fin(o, i, j):
                    RECIP(rl[0:1, 0, :], ps[0:1, 6, :], [pb[6]], [rl.t])
                    RECIP(rl[0:1, 1, :], ps[0:1, 7, :], [pb[7]], [rl.t])
                    TS("dve", rl[0:1, 1, :], rl[0:1, 1, :], nlam[0:1, 0:1], None, ALU.mult, None, [rl.t, nlam.t], [rl.t])
                    for m in range(2):
                        MM(ps[:, m, :], ones_f[0:1, :], rl[0:1, m, :], True, True, [ones_f.t, rl.t], [pb[m]])
                    ACT(bcs[:], ps[:, 0:2, :], AF.Identity, [pb[0], pb[1]], [bcs.t])
                    TT("dve", o1[:], ps[:, 4, :], bcs[:, 0, :], ALU.mult, [pb[4], bcs.t], [o1.t])
                    TT("dve", o2[:], ps[:, 5, :], bcs[:, 1, :], ALU.mult, [pb[5], bcs.t], [o2.t])
                    TT("pool", o1[:], o1[:], o2[:], ALU.add, [o1.t, o2.t], [o1.t])
                    TT("pool", sq[:], o1[:], o1[:], ALU.mult, [o1.t], [sq.t])
                    MM(ps[:, 2, :], ones_f[:, :], sq[:], True, True, [ones_f.t, sq.t], [pb[2]])
                    ACT(sq[:], ps[:, 2, :], AF.Sqrt, [pb[2], ceps.t], [sq.t], bias=ceps[:, 2:3], scale=1.0)
                    RECIP(sq[:], sq[:], [sq.t], [sq.t])
                    yb = ybt[cnt[0] % 2]
                    cnt[0] += 1
                    STT("dve", yb[:], o1[:], gvec[:, 0:1], sq[:], ALU.mult, ALU.mult, [o1.t, gvec.t, sq.t], [yb.t])
                    ST(CAT0T[512 + o * 128:512 + (o + 1) * 128, j * 512:(j + 1) * 512], yb[:], [yb.t], [tCAT0], yb.t)
                return fin

            attention(4, 1, QT0, tQT0, KT0sh, tKT0, load_v0, [128], 128, True, finalize0)
    cx.phase_barrier()
    if stop_after == "B":
        return done()

    def phase_outproj(CATT, tCAT, KC, Woutb_, tWout, Xres, tXres, g_in, b_in, Xo, tXo, XoT, tXoT, HLOC, tHLOC):
        with ExitStack() as st:
            wout = mk(st, "wout", [128, KC, 1024], BF16)
            LD(wout[:], Woutb_.rearrange("(kc p) n -> p kc n", p=128), [tWout], [wout.t], wout.t)
            gB = mk(st, "gB", [128, 1024], F32)
            bB = mk(st, "bB", [128, 1024], F32)
            LD(gB[:], g_in.partition_broadcast(128), [tIN], [gB.t], gB.t)
            LD(bB[:], b_in.partition_broadcast(128), [tIN], [bB.t], bB.t)
            cat = [mk(st, "cat", [128, KC, 512], BF16) for _ in range(2)]
            xr = [mk(st, "xr", [128, 1024], F32) for _ in range(2)]
            r = mk(st, "r", [128, 1024], F32)
            lnt = (mk(st, "st", [128, 12], F32), mk(st, "mv", [128, 2], F32), mk(st, "rs", [128, 1], F32),
                   mk(st, "tmp", [128, 1024], F32))
            xo = [mk(st, "xo", [128, 1024], F32) for _ in range(2)]
            xob = mk(st, "xob", [128, 1024], BF16)
            xoT = [mk(st, "xoT", [128, 8, 512], BF16) for _ in range(2)]

            def loadc(grp):
                c = cat[grp % 2]
                LD(c[:], CATT[:, grp * 512:(grp + 1) * 512].rearrange("(kc p) t -> p kc t", p=128), [tCAT], [c.t], c.t)

            def loadx(i):
                LD(xr[i % 2][:], Xres[i * 128:(i + 1) * 128, :], [tXres], [xr[i % 2].t], xr[i % 2].t)

            loadc(0)
            loadx(0)
            for i in range(NT):
                grp, sub = i // 4, i % 4
                if sub == 0 and grp + 1 < NG:
                    loadc(grp + 1)
                if i + 1 < NT:
                    loadx(i + 1)
                c = cat[grp % 2]
                yb = (i % 2) * 2
                for nb in range(2):
                    for kc in range(KC):
                        MM(ps[:, yb + nb, :], c[:, kc, sub * 128:(sub + 1) * 128], wout[:, kc, nb * 512:(nb + 1) * 512],
                           kc == 0, kc == KC - 1, [c.t, wout.t], [pb[yb + nb]])
                xi = xr[i % 2]
                STT("dve", r[:].rearrange("p (b n) -> p b n", b=2), xi[:].rearrange("p (b n) -> p b n", b=2), ALPHA,
                    ps[:, yb:yb + 2, :], ALU.mult, ALU.add, [xi.t, pb[yb], pb[yb + 1]], [r.t])
                xo_ = xo[i % 2]
                layernorm(lnt, r, xo_[:], gB, bB, [xo_.t])
                ST(Xo[i * 128:(i + 1) * 128, :], xo_[:], [xo_.t], [tXo], xo_.t)
                CP("dve", xob[:], xo_[:], [xo_.t], [xob.t])
                tb_ = 4 + (i % 2) * 2
                transposes(lambda j: xob[:, j * 128:(j + 1) * 128], 8, tb_, [xob.t])
                xt_ = xoT[grp % 2]
                ACT(xt_[:, :, sub * 128:(sub + 1) * 128], ps[:, tb_:tb_ + 2, :].rearrange("p b (k t) -> p (b k) t", t=128),
                    AF.Identity, [pb[tb_], pb[tb_ + 1]], [xt_.t])
                if sub == 3:
                    ST(XoT[:, grp * 512:(grp + 1) * 512].rearrange("(kc p) t -> p kc t", p=128), xt_[:], [xt_.t], [tXoT], xt_.t)
                    if grp == 0:
                        ST(HLOC[0], xt_[:, :, 0], [xt_.t], [tHLOC], xt_.t, slow=True)
                    if grp == NG - 1:
                        ST(HLOC[1], xt_[:, :, 511], [xt_.t], [tHLOC], xt_.t, slow=True)

    def phase_ffn(l, XT, tXT, HALOsh_, tHALO, Xres, tXres, g_in, b_in, Xo, tXo):
        with ExitStack() as st:
            wdn = mk(st, "wdn", [128, 22, 1024], BF16)
            LD(wdn[:], Wdnb[l].rearrange("(c p) n -> p c n", p=128), [tWdnb[l]], [wdn.t], wdn.t)
            cw = mk(st, "cw", [128, NFC, 3], F32)
            cbv = mk(st, "cbv", [128, NFC], F32)
            LD(cw[:], cw_in[l], [tIN], [cw.t], cw.t)
            LD(cbv[:], cbias_in[l], [tIN], [cbv.t], cbv.t)
            gB = mk(st, "gB", [128, 1024], F32)
            bB = mk(st, "bB", [128, 1024], F32)
            LD(gB[:], g_in.partition_broadcast(128), [tIN], [gB.t], gB.t)
            LD(bB[:], b_in.partition_broadcast(128), [tIN], [bB.t], bB.t)
            xT = [mk(st, "fxT", [128, 8, 514], BF16) for _ in range(2)]
            wu = [mk(st, "wu", [128, 2, 1024], BF16) for _ in range(3)]
            actT = [mk(st, "actT", [128, 22, 512], BF16) for _ in range(2)]
            cv = [[mk(st, "cv", [128, 512], F32) for _ in range(2)] for _ in range(2)]
            gt = [mk(st, "gt", [128, 512], F32) for _ in range(2)]
            xr = [mk(st, "xr", [128, 1024], F32) for _ in range(2)]
            r = mk(st, "r", [128, 1024], F32)
            lnt = (mk(st, "st", [128, 12], F32), mk(st, "mv", [128, 2], F32), mk(st, "rs", [128, 1], F32),
                   mk(st, "tmp", [128, 1024], F32))
            xo = [mk(st, "xo", [128, 1024], F32) for _ in range(2)]
            ph = [T("ph0"), T("ph1")]

            def loadxT(grp):
                x_ = xT[grp % 2]
                g0 = grp * 512
                LD(x_[:, :, 0:512], XT[:, g0:g0 + 512].rearrange("(kc p) t -> p kc t", p=128), [tXT], [x_.t], x_.t)
                if grp > 0:
                    LD(x_[:, :, 512:513], XT[:, g0 - 1:g0].rearrange("(kc p) t -> p kc t", p=128), [tXT], [x_.t], x_.t, slow=True)
                else:
                    LD(x_[:, :, 512], HALOsh_[0, 1], [tHALO], [x_.t], x_.t, slow=True)
                    TS("dve", x_[:, :, 512:513], x_[:, :, 512:513], flags[:, 2:3], None, ALU.mult, None,
                       [x_.t, flags.t], [x_.t])
                if grp < NG - 1:
                    LD(x_[:, :, 513:514], XT[:, g0 + 512:g0 + 513].rearrange("(kc p) t -> p kc t", p=128), [tXT], [x_.t], x_.t, slow=True)
                else:
                    LD(x_[:, :, 513], HALOsh_[1, 0], [tHALO], [x_.t], x_.t, slow=True)
                    TS("dve", x_[:, :, 513:514], x_[:, :, 513:514], flags[:, 3:4], None, ALU.mult, None,
                       [x_.t, flags.t], [x_.t])

            wcount = [0]

            def loadw(c):
                w_ = wu[wcount[0] % 3]
                wcount[0] += 1
                LD(w_[:, 0, :], WupRb[l][c], [tWupRb[l]], [w_.t], w_.t)
                LD(w_[:, 1, :], WupRb[l][c + 22], [tWupRb[l]], [w_.t], w_.t)
                return w_

            def loadx(i):
                LD(xr[i % 2][:], Xres[i * 128:(i + 1) * 128, :], [tXres], [xr[i % 2].t], xr[i % 2].t)

            loadxT(0)
            wq = [loadw(0), loadw(1)]
            it = 0
            for grp in range(NG):
                if grp + 1 < NG:
                    loadxT(grp + 1)
                x_ = xT[grp % 2]
                a_ = actT[grp % 2]
                for c in range(22):
                    w_ = wq.pop(0)
                    nxt = c + 2
                    if nxt < 22:
                        wq.append(loadw(nxt))
                    elif grp + 1 < NG:
                        wq.append(loadw(nxt - 22))
                    hb = (it % 2) * 2
                    for wh in range(2):
                        bank = hb + wh
                        hoff = (hb + wh) * 2
                        for kc in range(8):
                            MM(ps[:, bank, :], w_[:, wh, kc * 128:(kc + 1) * 128], x_[:, kc, 0:512], kc == 0, kc == 7,
                               [w_.t, x_.t], [pb[bank]])
                        for kc in range(8):
                            MM(ps[:, 4, hoff:hoff + 2], w_[:, wh, kc * 128:(kc + 1) * 128], x_[:, kc, 512:514], kc == 0, kc == 7,
                               [w_.t, x_.t], [ph[it % 2]])
                    for wh in range(2):
                        bank = hb + wh
                        hoff = (hb + wh) * 2
                        fc = c + 22 * wh
                        cv_ = cv[it % 2][wh]
                        ACT(cv_[:], ps[:, bank, :], AF.Identity, [pb[bank], cw.t, cbv.t], [cv_.t],
                            bias=cbv[:, fc:fc + 1], scale=cw[:, fc, 1:2])
                        STT("dve", cv_[:, 1:512], ps[:, bank, 0:511], cw[:, fc, 0:1], cv_[:, 1:512], ALU.mult, ALU.add,
                            [pb[bank], cw.t, cv_.t], [cv_.t])
                        STT("dve", cv_[:, 0:511], ps[:, bank, 1:512], cw[:, fc, 2:3], cv_[:, 0:511], ALU.mult, ALU.add,
                            [pb[bank], cw.t, cv_.t], [cv_.t])
                        STT("dve", cv_[:, 0:1], ps[:, 4, hoff:hoff + 1], cw[:, fc, 0:1], cv_[:, 0:1], ALU.mult, ALU.add,
                            [ph[it % 2], cw.t, cv_.t], [cv_.t])
                        STT("dve", cv_[:, 511:512], ps[:, 4, hoff + 1:hoff + 2], cw[:, fc, 2:3], cv_[:, 511:512], ALU.mult, ALU.add,
                            [ph[it % 2], cw.t, cv_.t], [cv_.t])
                    g_ = gt[it % 2]
                    ACT(g_[:], cv[it % 2][1][:], AF.Gelu_apprx_tanh, [cv[it % 2][1].t], [g_.t])
                    TT("pool", a_[:, c, :], g_[:], cv[it % 2][0][:], ALU.mult, [g_.t, cv[it % 2][0].t], [a_.t])
                    it += 1
                for sub in range(4):
                    i = grp * 4 + sub
                    loadx(i)
                    for nb in range(2):
                        for c in range(22):
                            MM(ps[:, 5 + nb, :], a_[:, c, sub * 128:(sub + 1) * 128], wdn[:, c, nb * 512:(nb + 1) * 512],
                               c == 0, c == 21, [a_.t, wdn.t], [pb[5 + nb]])
                    xi = xr[i % 2]
                    STT("dve", r[:].rearrange("p (b n) -> p b n", b=2), xi[:].rearrange("p (b n) -> p b n", b=2), ALPHA,
                        ps[:, 5:7, :], ALU.mult, ALU.add, [xi.t, pb[5], pb[6]], [r.t])
                    xo_ = xo[i % 2]
                    layernorm(lnt, r, xo_[:], gB, bB, [xo_.t])
                    ST(Xo[i * 128:(i + 1) * 128, :], xo_[:], [xo_.t], [tXo], xo_.t)

    if "C1" not in SKIP:
        phase_outproj(CAT0T, tCAT0, 8, Wout0b, tWout0b, x_in, tIN, lnp_in["ln0a_g"], lnp_in["ln0a_b"], X1, tX1, X1T, tX1T,
                      LOC2.rearrange("(w p k) -> w p k", w=2, p=128), tLOC2)
    if stop_after == "C1":
        return done()
    publish(LOC2, tLOC2, SH2, tSH2)
    if "C2" not in SKIP:
        phase_ffn(0, X1T, tX1T, SH2.rearrange("s (w p k) -> s w p k", w=2, p=128), tSH2, X1, tX1, lnp_in["ln0b_g"], lnp_in["ln0b_b"], X2, tX2)
    cx.phase_barrier()

    if stop_after == "C2":
        return done()

    if "D0" not in SKIP:
        with ExitStack() as st:
            w_in = mk(st, "w_in1", [128, 8, 3072], BF16)
            LD(w_in[:], Win1b.rearrange("(kc p) n -> p kc n", p=128), [tWin1b], [w_in.t], w_in.t)
            qng = mk(st, "qng", [128, 64], F32)
            kng = mk(st, "kng", [128, 64], F32)
            LD(qng[:], qng_in.partition_broadcast(128), [tIN], [qng.t], qng.t)
            LD(kng[:], kng_in.partition_broadcast(128), [tIN], [kng.t], kng.t)
            xin = [mk(st, "xin", [128, 1024], F32) for _ in range(2)]
            tabt = [mk(st, "tab", [128, 80], F32) for _ in range(2)]
            xb = mk(st, "xb", [128, 1024], BF16)
            xT = mk(st, "xT", [128, 2, 512], BF16)
            sqt = mk(st, "sqt", [128, 640], F32)
            ss = mk(st, "ss", [128, 10], F32)
            rsq = mk(st, "rsq", [128, 10], F32)
            qn = mk(st, "qn", [128, 640], F32)
            rt = mk(st, "rt", [128, 4, 320], F32)
            qkr = mk(st, "qkr", [128, 768], BF16)
            qkd = mk(st, "qkd", [128, 1536], BF16)
            rtmp = mk(st, "rtmp", [128, 4, 192], F32)
            v1s = [mk(st, "v1s", [128, 2, 65], BF16) for _ in range(2)]
            vds = [mk(st, "vds", [128, 12, 65], BF16) for _ in range(2)]
            Q1s = [mk(st, "Q1s", [128, 4, 512], BF16) for _ in range(2)]
            K1s = [mk(st, "K1s", [128, 2, 512], BF16) for _ in range(2)]
            Qds = [mk(st, "Qds", [128, 6, 512], BF16) for _ in range(2)]
            Kds = [mk(st, "Kds", [128, 6, 512], BF16) for _ in range(2)]
            for b in range(2):
                cx.op("dve", lambda e: e.memset(v1s[b][:], 1.0), [], [v1s[b].t])
                cx.op("dve", lambda e: e.memset(vds[b][:], 1.0), [], [vds[b].t])

            def loadD(i):
                LD(xin[i % 2][:], X2[i * 128:(i + 1) * 128, :], [tX2], [xin[i % 2].t], xin[i % 2].t)
                LD(tabt[i % 2][:], tab_in[i * 128:(i + 1) * 128, :], [tIN], [tabt[i % 2].t], tabt[i % 2].t)

            def psflat(c0, c1):
                b0, b1 = c0 // 512, (c1 - 1) // 512
                assert b0 == b1
                return ps[:, 2 + b0, c0 - b0 * 512:c1 - b0 * 512]

            loadD(0)
            for i in range(NT):
                if i + 1 < NT:
                    loadD(i + 1)
                sub, grp = i % 4, i // 4
                g0 = grp * 512
                xi, tb = xin[i % 2], tabt[i % 2]
                CP("pool", xb[:], xi[:], [xi.t], [xb.t])
                transposes(lambda j: xb[:, j * 128:(j + 1) * 128], 8, 0, [xb.t])
                ACT(xT[:], ps[:, 0:2, :], AF.Identity, [pb[0], pb[1]], [xT.t])
                for cb in range(6):
                    for kc in range(8):
                        MM(ps[:, 2 + cb, :], xT[:, kc // 4, (kc % 4) * 128:(kc % 4) * 128 + 128],
                           w_in[:, kc, cb * 512:(cb + 1) * 512], kc == 0, kc == 7, [xT.t, w_in.t], [pb[2 + cb]])
                ACT(sqt[:, 0:512], ps[:, 2, :], AF.Square, [pb[2]], [sqt.t])
                ACT(sqt[:, 512:640], ps[:, 3, 0:128], AF.Square, [pb[3]], [sqt.t])
                cx.op("dve", lambda e: e.tensor_reduce(out=ss[:, 0:10], in_=sqt[:].rearrange("p (h d) -> p h d", d=64),
                                                       axis=AX.X, op=ALU.add), [sqt.t], [ss.t])
                ACT(rsq[:], ss[:], AF.Sqrt, [ss.t, ceps.t], [rsq.t], bias=ceps[:, 1:2], scale=1.0 / 64.0)
                RECIP(rsq[:], rsq[:], [rsq.t], [rsq.t])
                TT("dve", qn[:, 0:512].rearrange("p (h d) -> p h d", d=64), ps[:, 2, :].rearrange("p (h d) -> p h d", d=64),
                   rsq[:, 0:8].unsqueeze(2).to_broadcast([128, 8, 64]), ALU.mult, [pb[2], rsq.t], [qn.t])
                TT("dve", qn[:, 512:640].rearrange("p (h d) -> p h d", d=64), ps[:, 3, 0:128].rearrange("p (h d) -> p h d", d=64),
                   rsq[:, 8:10].unsqueeze(2).to_broadcast([128, 2, 64]), ALU.mult, [pb[3], rsq.t], [qn.t])
                TT("pool", qn[:, 0:512].rearrange("p (h d) -> p h d", d=64), qn[:, 0:512].rearrange("p (h d) -> p h d", d=64),
                   qng[:, :].unsqueeze(1).to_broadcast([128, 8, 64]), ALU.mult, [qn.t, qng.t], [qn.t])
                TT("pool", qn[:, 512:640].rearrange("p (h d) -> p h d", d=64), qn[:, 512:640].rearrange("p (h d) -> p h d", d=64),
                   kng[:, :].unsqueeze(1).to_broadcast([128, 2, 64]), ALU.mult, [qn.t, kng.t], [qn.t])
                q5 = qn[:].rearrange("p (h a b d) -> p h a b d", a=2, b=2, d=16)
                x1 = q5[:, :, :, 0, :]
                x2 = q5[:, :, :, 1, :]
                cosb = tb[:, 16:48].rearrange("p (a d) -> p a d", d=16).unsqueeze(1).to_broadcast([128, 10, 2, 16])
                sinb = tb[:, 48:80].rearrange("p (a d) -> p a d", d=16).unsqueeze(1).to_broadcast([128, 10, 2, 16])
                t4 = [rt[:, k, :].rearrange("p (h a d) -> p h a d", a=2, d=16) for k in range(4)]
                TT("dve", t4[0], x1, cosb, ALU.mult, [qn.t, tb.t], [rt.t])
                TT("pool", t4[1], x2, sinb, ALU.mult, [qn.t, tb.t], [rt.t])
                TT("dve", t4[2], x2, cosb, ALU.mult, [qn.t, tb.t], [rt.t])
                TT("pool", t4[3], x1, sinb, ALU.mult, [qn.t, tb.t], [rt.t])
                qo5 = qkr[:, 0:512].rearrange("p (h a b d) -> p h a b d", a=2, b=2, d=16)
                TT("dve", qo5[:, :, :, 0, :], t4[0][:, 0:8], t4[1][:, 0:8], ALU.subtract, [rt.t], [qkr.t])
                TT("dve", qo5[:, :, :, 1, :], t4[2][:, 0:8], t4[3][:, 0:8], ALU.add, [rt.t], [qkr.t])
                ko = qkr[:, 512:768].rearrange("p (kv u a b d) -> p kv u a b d", u=2, a=2, b=2, d=16)
                for u_ in range(2):
                    TT("pool", ko[:, :, u_, :, 0, :], t4[0][:, 8:10], t4[1][:, 8:10], ALU.subtract, [rt.t], [qkr.t])
                    TT("pool", ko[:, :, u_, :, 1, :], t4[2][:, 8:10], t4[3][:, 8:10], ALU.add, [rt.t], [qkr.t])
                v1 = v1s[i % 2]
                CP("dve", v1[:, :, 0:64], ps[:, 3, 128:256].rearrange("p (kv d) -> p kv d", d=64), [pb[3]], [v1.t])
                ST(V1loc[i * 128:(i + 1) * 128, :], v1[:].rearrange("p kv e -> p (kv e)"), [v1.t], [tLOC3], v1.t)
                ACT(qkd[:, 0:256], ps[:, 3, 256:512], AF.Identity, [pb[3]], [qkd.t])
                ACT(qkd[:, 256:1280].rearrange("p (b n) -> p b n", b=2), ps[:, 4:6, :], AF.Identity, [pb[4], pb[5]], [qkd.t])
                ACT(qkd[:, 1280:1536], ps[:, 6, 0:256], AF.Identity, [pb[6]], [qkd.t])
                for (c0, c1, bk, o0) in ((0, 256, 3, 256), (256, 768, 4, 0), (768, 1280, 5, 0), (1280, 1536, 6, 0)):
                    nh = (c1 - c0) // 64
                    src3 = ps[:, bk, o0:o0 + (c1 - c0)].rearrange("p (h d) -> p h d", d=64)
                    dst3 = qkd[:, c0:c1].rearrange("p (h d) -> p h d", d=64)
                    rope16(src3, dst3, nh, tb[:, 0:8], tb[:, 8:16], rtmp, [pb[bk], tb.t], [qkd.t])
                vd = vds[i % 2]
                CP("dve", vd[:, 0:4, 0:64], ps[:, 6, 256:512].rearrange("p (h d) -> p h d", d=64), [pb[6]], [vd.t])
                CP("dve", vd[:, 4:12, 0:64], ps[:, 7, :].rearrange("p (h d) -> p h d", d=64), [pb[7]], [vd.t])
                vd_flat = vd[:].rearrange("p h e -> p (h e)")
                ST(VdP[HALO + i * 128:HALO + (i + 1) * 128, :], vd_flat, [vd.t], [tVdP], vd.t)
                if i * 128 < HALO:
                    ST(VdHloc[0, i * 128:(i + 1) * 128, :], vd_flat, [vd.t], [tLOC3], vd.t)
                if i * 128 >= Tk - HALO:
                    o_ = i * 128 - (Tk - HALO)
                    ST(VdHloc[1, o_:o_ + 128, :], vd_flat, [vd.t], [tLOC3], vd.t)
                q1, k1, qd_, kd_ = Q1s[grp % 2], K1s[grp % 2], Qds[grp % 2], Kds[grp % 2]
                transposes(lambda j: qkr[:, j * 128:(j + 1) * 128], 6, 0, [qkr.t])
                CP("dve", q1[:, :, sub * 128:(sub + 1) * 128], ps[:, 0, :].rearrange("p (h t) -> p h t", t=128), [pb[0]], [q1.t])
                CP("act", k1[:, :, sub * 128:(sub + 1) * 128], ps[:, 1, 0:256].rearrange("p (h t) -> p h t", t=128), [pb[1]], [k1.t])
                transposes(lambda j: qkd[:, j * 128:(j + 1) * 128], 6, 2, [qkd.t])
                CP("dve", qd_[:, 0:4, sub * 128:(sub + 1) * 128], ps[:, 2, :].rearrange("p (h t) -> p h t", t=128), [pb[2]], [qd_.t])
                CP("act", qd_[:, 4:6, sub * 128:(sub + 1) * 128], ps[:, 3, 0:256].rearrange("p (h t) -> p h t", t=128), [pb[3]], [qd_.t])
                transposes(lambda j: qkd[:, 768 + j * 128:768 + (j + 1) * 128], 6, 4, [qkd.t])
                CP("dve", kd_[:, 0:4, sub * 128:(sub + 1) * 128], ps[:, 4, :].rearrange("p (h t) -> p h t", t=128), [pb[4]], [kd_.t])
                CP("act", kd_[:, 4:6, sub * 128:(sub + 1) * 128], ps[:, 5, 0:256].rearrange("p (h t) -> p h t", t=128), [pb[5]], [kd_.t])
                if sub == 3:
                    ST(Q1T[:, :, g0:g0 + 512].rearrange("h p t -> p h t"), q1[:], [q1.t], [tQ1T], q1.t)
                    ST(K1Tloc[:, :, g0:g0 + 512].rearrange("k p t -> p k t"), k1[:], [k1.t], [tLOC3], k1.t)
                    ST(QdT[:, :, g0:g0 + 512].rearrange("h p t -> p h t"), qd_[:], [qd_.t], [tQdT], qd_.t)
                    ST(KdTP[:, :, HALO + g0:HALO + g0 + 512].rearrange("h p t -> p h t"), kd_[:], [kd_.t], [tKdTP], kd_.t)
                    if g0 < HALO:
                        ST(KdTHloc[0, :, :, g0:g0 + 512].rearrange("c p t -> p c t"), kd_[:], [kd_.t], [tLOC3], kd_.t)
                    if g0 >= Tk - HALO:
                        o_ = g0 - (Tk - HALO)
                        ST(KdTHloc[1, :, :, o_:o_ + 512].rearrange("c p t -> p c t"), kd_[:], [kd_.t], [tLOC3], kd_.t)
    if stop_after == "D0":
        return done()
    publish(LOC3, tLOC3, SH3, tSH3)

    def load_v1(o, vtile):
        for s in range(2):
            for c0 in range(0, NT, 16):
                LD(vtile[:, s * NT + c0:s * NT + c0 + 16, :],
                   V1sh[s, c0 * 128:(c0 + 16) * 128, o * 65:(o + 1) * 65].rearrange("(b p) n -> p b n", p=128),
                   [tV1], [vtile.t], vtile.t)

    def finalize1(st):
        rl = mk(st, "rl1", [128, 2, 512], F32)
        bcs = mk(st, "bcs1", [64, 2, 512], F32)
        yc = [mk(st, "yc", [64, 2, 512], BF16) for _ in range(2)]
        cnt = [0]

        def fin(o, i, j):
            for m in range(2):
                RECIP(rl[64:65, m, :], ps[64:65, 4 + m, :], [pb[4 + m]], [rl.t])
                MM(ps[0:64, m, :], ones_f[64:65, 0:64], rl[64:65, m, :], True, True, [ones_f.t, rl.t], [pb[m]])
            ACT(bcs[:], ps[0:64, 0:2, :], AF.Identity, [pb[0], pb[1]], [bcs.t])
            y_ = yc[cnt[0] % 2]
            cnt[0] += 1
            TT("dve", y_[:], ps[0:64, 4:6, :], bcs[:], ALU.mult, [pb[4], pb[5], bcs.t], [y_.t])
            for m in range(2):
                hd = o * 4 + i * 2 + m
                ST(CAT1T[hd * 64:(hd + 1) * 64, j * 512:(j + 1) * 512], y_[:, m, :], [y_.t], [tCAT1], y_.t)
        return fin

    if "D1" not in SKIP:
        attention(2, 2, Q1T, tQ1T, K1Tsh, tK1T, load_v1, [65], 65, False, finalize1)
    cx.phase_barrier()
    if stop_after == "D1":
        return done()

    if "D2" not in SKIP:
        with ExitStack() as st:
            maskab = mk(st, "maskab", [128, 2, 128], BF16)
            LD(maskab[:], mask_in, [tIN], [maskab.t], maskab.t, q="pool")
            hv = mk(st, "hv", [128, 8, 780], BF16)
            for side, (srcslot, which, flagc, row0) in enumerate(((0, 1, 2, 0), (1, 0, 3, HALO + Tk))):
                LD(hv[:], VdHsh[srcslot, which].rearrange("(a p) n -> p a n", p=128), [tVdH], [hv.t], hv.t)
                TS("dve", hv[:], hv[:], flags[:, flagc:flagc + 1], None, ALU.mult, None, [hv.t, flags.t], [hv.t])
                ST(VdP[row0:row0 + HALO, :].rearrange("(a p) n -> p a n", p=128), hv[:], [hv.t], [tVdP], hv.t)
            hk = mk(st, "hk", [128, 6, HALO], BF16)
            for side, (srcslot, which, col0) in enumerate(((0, 1, 0), (1, 0, HALO + Tk))):
                LD(hk[:], KdTHsh[srcslot, which].rearrange("c p t -> p c t"), [tKdTH], [hk.t], hk.t)
                ST(KdTP[:, :, col0:col0 + HALO].rearrange("c p t -> p c t"), hk[:], [hk.t], [tKdTP], hk.t)
            qd = [mk(st, "qd", [128, 2, 2048], BF16) for _ in range(2)]
            kd = [mk(st, "kd", [128, 2, 4096], BF16) for _ in range(2)]
            vb = [mk(st, "vb", [128, 2, 4, 65], BF16) for _ in range(3)]
            pd = [mk(st, "pd", [128, 2, 128], BF16) for _ in range(3)]
            acc = mk(st, "acc", [65, 4, 2048], F32)
            rl = mk(st, "rld", [128, 512], F32)
            bcs = mk(st, "bcsd", [64, 512], F32)
            ydT = [mk(st, "ydT", [64, 512], BF16) for _ in range(2)]
            itq = 0
            itv = 0
            itp = 0
            ito = 0
            for ST_ in range(NST):
                t0 = ST_ * 2048
                for p, (wnd, r) in enumerate(PATTERNS):
                    q_, k_ = qd[itq % 2], kd[itq % 2]
                    itq += 1
                    for hp in range(2):
                        LD(q_[:, hp, :], QdT[p * 2 + hp, :, t0:t0 + 2048], [tQdT], [q_.t], q_.t)
                        LD(k_[:, hp, :], KdTP[p * 2 + hp, :, t0:t0 + 4096], [tKdTP], [k_.t], k_.t)
                    nblk = 16 // r
                    for cl in range(r):
                        for blk in range(nblk):
                            q0 = cl + r * blk * 128
                            row0 = t0 + q0 + HALO - 64 * r
                            v_ = vb[itv % 3]
                            itv += 1
                            LD(v_[:].rearrange("p a h e -> p a (h e)"),
                               VdP[row0:row0 + 255 * r + 1:r, p * 260:(p + 1) * 260].rearrange("(a k) e -> k a e", a=2),
                               [tVdP], [v_.t], v_.t)
                            for hp in range(2):
                                for m in range(2):
                                    h = hp * 2 + m
                                    sbk = itp % 4
                                    p_ = pd[itp % 3]
                                    itp += 1
                                    for half in range(2):
                                        k0 = HALO + q0 - 64 * r + 128 * r * half
                                        MM(ps[:, sbk, half * 128:(half + 1) * 128],
                                           k_[64 * m:64 * m + 64, hp, k0:k0 + 127 * r + 1:r],
                                           q_[64 * m:64 * m + 64, hp, q0:q0 + 127 * r + 1:r], True, True, [k_.t, q_.t], [pb[sbk]])
                                    ACT(p_[:].rearrange("p a k -> p (a k)"), ps[:, sbk, 0:256], AF.Exp, [pb[sbk]], [p_.t], scale=0.125)
                                    TT("pool", p_[:], p_[:], maskab[:], ALU.mult, [p_.t, maskab.t], [p_.t])
                                    obk = 4 + ito % 2
                                    ito += 1
                                    for half in range(2):
                                        MM(ps[0:65, obk, 0:128], v_[:, half, h, :], p_[:, half, :], half == 0, half == 1,
                                           [v_.t, p_.t], [pb[obk]])
                                    av = acc[0:65, h, q0:q0 + 127 * r + 1:r]
                                    if p == 0:
                                        CP("dve", av, ps[0:65, obk, 0:128], [pb[obk]], [acc.t])
                                    else:
                                        TT("dve", av, av, ps[0:65, obk, 0:128], ALU.add, [pb[obk], acc.t], [acc.t])
                for h in range(4):
                    for qc in range(4):
                        RECIP(rl[64:65, :], acc[64:65, h, qc * 512:(qc + 1) * 512], [acc.t], [rl.t])
                        MM(ps[0:64, 6, :], ones_f[64:65, 0:64], rl[64:65, :], True, True, [ones_f.t, rl.t], [pb[6]])
                        ACT(bcs[:], ps[0:64, 6, :], AF.Identity, [pb[6]], [bcs.t])
                        y_ = ydT[(h * 4 + qc) % 2]
                        TT("dve", y_[:], acc[0:64, h, qc * 512:(qc + 1) * 512], bcs[:], ALU.mult, [acc.t, bcs.t], [y_.t])
                        ST(CAT1T[512 + h * 64:512 + (h + 1) * 64, t0 + qc * 512:t0 + (qc + 1) * 512], y_[:], [y_.t], [tCAT1], y_.t)
    cx.phase_barrier()

    if stop_after == "D2":
        return done()
    _ev = _os.environ.get("KDBG_EVAR", "")
    if _ev == "1":
        phase_outproj(CAT0T, tCAT0, 8, Wout0b, tWout0b, X2, tX2, lnp_in["ln1a_g"], lnp_in["ln1a_b"], X3, tX3, X3T, tX3T,
                      LOC4.rearrange("(w p k) -> w p k", w=2, p=128), tLOC4)
    elif _ev == "2":
        phase_outproj(CAT1T, tCAT1, 6, Wout1b, tWout1b, x_in, tIN, lnp_in["ln1a_g"], lnp_in["ln1a_b"], X3, tX3, X3T, tX3T,
                      LOC4.rearrange("(w p k) -> w p k", w=2, p=128), tLOC4)
    elif _ev == "3":
        phase_outproj(CAT1T, tCAT1, 6, Wout1b, tWout1b, X2, tX2, lnp_in["ln1a_g"], lnp_in["ln1a_b"], X1, tX1, X1T, tX1T,
                      LOC2.rearrange("(w p k) -> w p k", w=2, p=128), tLOC2)
    else:
        phase_outproj(CAT1T, tCAT1, 6, Wout1b, tWout1b, X2, tX2, lnp_in["ln1a_g"], lnp_in["ln1a_b"], X3, tX3, X3T, tX3T,
                      LOC4.rearrange("(w p k) -> w p k", w=2, p=128), tLOC4)
    if stop_after == "E":
        return done()
    publish(LOC4, tLOC4, SH4, tSH4)
    if stop_after == "E2":
        return done()
    phase_ffn(1, X3T, tX3T, SH4.rearrange("s (w p k) -> s w p k", w=2, p=128), tSH4, X3, tX3, lnp_in["ln1b_g"], lnp_in["ln1b_b"], y_out, tY)
    cx.drain("pool", [tY])
    return done()


def _rope_tab(pos, n_dims, theta):
    inv = np.power(np.float32(theta), -np.arange(0, n_dims, 2, dtype=np.float32) / np.float32(n_dims)).astype(np.float32)
    ang = pos.astype(np.float32)[:, None] * inv[None, :]
    return np.cos(ang).astype(np.float32), np.sin(ang).astype(np.float32)


_CACHE = {}


def kernel(**inp):
    f32 = np.float32
    x_prompt = np.asarray(inp["x_prompt"], f32)
    x_sample = np.asarray(inp["x_sample"], f32)
    Tk = x_prompt.shape[1]
    assert x_prompt.shape[0] == 4 and x_sample.shape[0] == 2 and x_sample.shape[1] == 2 * Tk
    if Tk not in _CACHE:
        _CACHE[Tk] = build_program(Tk)[0]
    nc = _CACHE[Tk]

    def A(k):
        return np.ascontiguousarray(np.asarray(inp[k], f32))

    shared = {
        "ident": np.eye(128, dtype=f32),
        "w_in0": A("w_in0"),
        "wsT": np.ascontiguousarray(A("a_ws").transpose(2, 0, 1)),
        "bsT": np.ascontiguousarray(A("a_bs").T),
        "a_ln_g": A("a_ln_g"), "a_ln_b": A("a_ln_b"),
        "lam": np.ascontiguousarray(np.concatenate([A("b_lam_q1"), A("b_lam_q2"), A("b_lam_k1"), A("b_lam_k2")])[None, :]),
        "subg": np.ascontiguousarray(A("b_subln_g")[:, None]),
        "w_out0": A("w_out0"), "w_out1": A("w_out1"), "w_in1": A("w_in1"),
        "qng": A("c_qnorm_g"), "kng": A("c_knorm_g"),
    }
    for nm in ("ln0a_g", "ln0a_b", "ln0b_g", "ln0b_b", "ln1a_g", "ln1a_b", "ln1b_g", "ln1b_b"):
        shared[nm] = A(nm)
    for l in range(2):
        wu = A("w_up%d" % l).reshape(8, 128, NFC, 128).transpose(2, 1, 0, 3)
        shared["w_upR%d" % l] = np.ascontiguousarray(wu.reshape(NFC, 128, 1024))
        shared["convwT%d" % l] = np.ascontiguousarray(A("conv_w%d" % l).reshape(3, NFC, 128).transpose(2, 1, 0))
        shared["convbT%d" % l] = np.ascontiguousarray(A("conv_b%d" % l).reshape(NFC, 128).T)
        shared["w_down%d" % l] = A("w_down%d" % l)
    kk = np.arange(128)[:, None]
    qq = np.arange(128)[None, :]
    shared["maskab"] = np.ascontiguousarray(np.stack([(kk >= qq), (qq >= kk)], axis=1).astype(f32))

    in_maps = []
    for c in range(8):
        if c < 4:
            x = x_prompt[c]
            pos0 = 0
            sl = c % 2
            b = [NEG, NEG]
            b[sl] = 0.0
            fl, fr = 0.0, 0.0
        else:
            seq, half = (c - 4) // 2, (c - 4) % 2
            x = x_sample[seq, half * Tk:(half + 1) * Tk]
            pos0 = half * Tk
            b = [0.0, 0.0]
            fl = 1.0 if half == 1 else 0.0
            fr = 1.0 if half == 0 else 0.0
        pos = pos0 + np.arange(Tk)
        cp, sp = _rope_tab(pos, 16, 500000.0)
        cr, sr = _rope_tab(pos // 64, 32, 10000.0)
        cc, sc = _rope_tab(pos % 64, 32, 10000.0)
        tab = np.concatenate([cp, sp, cr, cc, sr, sc], axis=1).astype(f32)
        flags = np.zeros((128, 8), f32)
        flags[:, 0] = b[0]
        flags[:, 1] = b[1]
        flags[:, 2] = fl
        flags[:, 3] = fr
        m = dict(shared)
        m["x"] = np.ascontiguousarray(x)
        m["tab"] = np.ascontiguousarray(tab)
        m["flags"] = flags
        in_maps.append(m)

    res = run_bass_kernel_spmd(nc, in_maps, core_ids=list(range(8)))
    outs = [np.asarray(r["y"], f32) for r in res.results]
    y_prompt = np.stack(outs[0:4], axis=0)
    y_sample = np.stack([np.concatenate(outs[4:6], axis=0), np.concatenate(outs[6:8], axis=0)], axis=0)
    return (y_prompt, y_sample)
```
